# Optimizing a Trainium2 kernel written in Bass

```python
import math
import jax, jax.numpy as jnp
from jax import lax
import numpy as np

D_MODEL = 2048
BATCH = 4
SEQ = 8192
DEPTH = 4

CHUNK = 64
N_META = 16
Q_BLOCK = 128
MLA_HEADS = 16
QK_NOPE = 128
QK_ROPE = 64
V_DIM = 128
KV_RANK = 512
ROPE_BASE = 10000.0
MLA_SCALE = (QK_NOPE + QK_ROPE) ** -0.5
CONV_DIM = 2048
CONV_K = 31
SSM_DIM = 2048
SSM_GROUP = 16
SSM_GROUPS = SSM_DIM // SSM_GROUP
SSM_STATE = 64
FFN_DIM = 256 * ((8 * D_MODEL + 3 * 256 - 1) // (3 * 256))
DEEPNORM_ALPHA = (2 * DEPTH) ** 0.25
DEEPNORM_BETA = (8 * DEPTH) ** -0.25
IN_SIZES = (
    MLA_HEADS * (QK_NOPE + QK_ROPE),
    KV_RANK,
    QK_ROPE,
    CONV_DIM,
    CONV_DIM,
    SSM_DIM,
    D_MODEL,
    D_MODEL,
    D_MODEL,
)
IN_OFFSETS = tuple(int(v) for v in np.cumsum((0,) + IN_SIZES))
N_IN = IN_OFFSETS[-1]

kernel_name = "hybrid_mla_conformer_s5_deepnorm"


def _layer_norm(x, g, b, eps=1e-5):
    xf = x.astype(jnp.float32)
    mu = jnp.mean(xf, axis=-1, keepdims=True)
    var = jnp.mean(jnp.square(xf - mu), axis=-1, keepdims=True)
    y = (xf - mu) * lax.rsqrt(var + eps) * g.astype(jnp.float32) + b.astype(jnp.float32)
    return y.astype(x.dtype)


def _rms_norm(x, g, eps=1e-6):
    xf = x.astype(jnp.float32)
    y = xf * lax.rsqrt(jnp.mean(jnp.square(xf), axis=-1, keepdims=True) + eps) * g.astype(jnp.float32)
    return y.astype(x.dtype)


def _rope_tables(length):
    half = QK_ROPE // 2
    inv = jnp.power(ROPE_BASE, -jnp.arange(half, dtype=jnp.float32) / half)
    ang = jnp.arange(length, dtype=jnp.float32)[:, None] * inv[None, :]
    return jnp.cos(ang), jnp.sin(ang)


def _apply_rope(x, cos, sin):
    xf = x.astype(jnp.float32)
    x1, x2 = jnp.split(xf, 2, axis=-1)
    return jnp.concatenate([x1 * cos - x2 * sin, x1 * sin + x2 * cos], axis=-1).astype(x.dtype)


def _chunk_ids(length):
    pos = jnp.arange(length)
    return jnp.where(pos < N_META, 0, (pos - N_META) // CHUNK + 1)


def _mla_attend(qn, qr, q_cid, kn, kr, v, k_cid):
    s = jnp.einsum('bqhd,bkhd->bhqk', qn, kn) + jnp.einsum('bqhr,bkr->bhqk', qr, kr)
    s = s.astype(jnp.float32) * MLA_SCALE
    visible = k_cid[None, :] <= q_cid[:, None]
    s = jnp.where(visible[None, None], s, -jnp.inf)
    p = jax.nn.softmax(s, axis=-1).astype(v.dtype)
    return jnp.einsum('bhqk,bkhd->bqhd', p, v)


def _mla_branch(q, c_kv, k_rope, kv_norm_g, w_ukv, w_mla_out, cos, sin, cid):
    bsz, length, _ = q.shape
    q = q.reshape(bsz, length, MLA_HEADS, QK_NOPE + QK_ROPE)
    qn = q[..., :QK_NOPE]
    qr = _apply_rope(q[..., QK_NOPE:], cos[None, :, None, :], sin[None, :, None, :])
    kr = _apply_rope(k_rope, cos[None], sin[None])
    kv = (_rms_norm(c_kv, kv_norm_g) @ w_ukv).reshape(bsz, length, MLA_HEADS, QK_NOPE + V_DIM)
    kn, v = kv[..., :QK_NOPE], kv[..., QK_NOPE:]
    o_meta = _mla_attend(qn[:, :N_META], qr[:, :N_META], cid[:N_META],
                         kn[:, :N_META], kr[:, :N_META], v[:, :N_META], cid[:N_META])
    n_blk = (length - N_META) // Q_BLOCK

    def query_block(i):
        start = N_META + i * Q_BLOCK
        qn_b = lax.dynamic_slice_in_dim(qn, start, Q_BLOCK, axis=1)
        qr_b = lax.dynamic_slice_in_dim(qr, start, Q_BLOCK, axis=1)
        cid_b = lax.dynamic_slice_in_dim(cid, start, Q_BLOCK, axis=0)
        return _mla_attend(qn_b, qr_b, cid_b, kn, kr, v, cid)

    o_real = lax.map(query_block, jnp.arange(n_blk))
    o_real = jnp.moveaxis(o_real, 0, 1).reshape(bsz, length - N_META, MLA_HEADS * V_DIM)
    o = jnp.concatenate([o_meta.reshape(bsz, N_META, MLA_HEADS * V_DIM), o_real], axis=1)
    return o @ w_mla_out


def _conv_branch(val, gate, conv_w, conv_b, ln_g, ln_b, w_conv_out):
    z = val * jax.nn.sigmoid(gate)
    z = lax.conv_general_dilated(z, conv_w[:, None, :], window_strides=(1,),
                                 padding=[(CONV_K - 1, 0)],
                                 dimension_numbers=('NWC', 'WIO', 'NWC'),
                                 feature_group_count=CONV_DIM) + conv_b
    z = jax.nn.silu(_layer_norm(z, ln_g, ln_b))
    return z @ w_conv_out


def _ssm_combine(e1, e2):
    a1r, a1i, b1r, b1i = e1
    a2r, a2i, b2r, b2i = e2
    return (a2r * a1r - a2i * a1i, a2r * a1i + a2i * a1r,
            a2r * b1r - a2i * b1i + b2r, a2r * b1i + a2i * b1r + b2i)


def _ssm_segment(h_re, h_im, u_seg, a_re, a_im, bb_re, bb_im, c_re, c_im):
    d_re = jnp.einsum('btgc,gpc->btgp', u_seg, bb_re)
    d_im = jnp.einsum('btgc,gpc->btgp', u_seg, bb_im)
    ar = jnp.broadcast_to(a_re, d_re.shape)
    ai = jnp.broadcast_to(a_im, d_re.shape)
    cum_re, cum_im, x_re, x_im = lax.associative_scan(_ssm_combine, (ar, ai, d_re, d_im), axis=1)
    x_re2 = x_re + cum_re * h_re[:, None] - cum_im * h_im[:, None]
    x_im2 = x_im + cum_re * h_im[:, None] + cum_im * h_re[:, None]
    y = jnp.einsum('btgp,gcp->btgc', x_re2, c_re) - jnp.einsum('btgp,gcp->btgc', x_im2, c_im)
    return x_re2[:, -1], x_im2[:, -1], y


def _ssm_branch(u, lam_re, lam_im, log_dt, b_re, b_im, c_re, c_im, d_skip, w_glu):
    f32 = jnp.float32
    bsz, length, _ = u.shape
    lam_re = lam_re.astype(f32)
    lam_im = lam_im.astype(f32)
    dt = jnp.exp(log_dt.astype(f32))[:, None]
    mag = jnp.exp(lam_re * dt)
    ang = lam_im * dt
    a_re, a_im = mag * jnp.cos(ang), mag * jnp.sin(ang)
    den = jnp.square(lam_re) + jnp.square(lam_im)
    f_re = ((a_re - 1.0) * lam_re + a_im * lam_im) / den
    f_im = (a_im * lam_re - (a_re - 1.0) * lam_im) / den
    br, bi = b_re.astype(f32), b_im.astype(f32)
    bb_re = f_re[..., None] * br - f_im[..., None] * bi
    bb_im = f_re[..., None] * bi + f_im[..., None] * br
    cr, ci = c_re.astype(f32), c_im.astype(f32)
    uf = u.astype(f32)
    ug = uf.reshape(bsz, length, SSM_GROUPS, SSM_GROUP)
    h0 = jnp.zeros((bsz, SSM_GROUPS, SSM_STATE), f32)
    h_re, h_im, y_meta = _ssm_segment(h0, h0, ug[:, :N_META], a_re, a_im, bb_re, bb_im, cr, ci)
    n_chunk = (length - N_META) // CHUNK
    u_chunks = jnp.moveaxis(ug[:, N_META:].reshape(bsz, n_chunk, CHUNK, SSM_GROUPS, SSM_GROUP), 1, 0)

    def step(carry, uc):
        hr, hi, yc = _ssm_segment(carry[0], carry[1], uc, a_re, a_im, bb_re, bb_im, cr, ci)
        return (hr, hi), yc

    _, y_real = lax.scan(step, (h_re, h_im), u_chunks)
    y_real = jnp.moveaxis(y_real, 0, 1).reshape(bsz, length - N_META, SSM_DIM)
    y = jnp.concatenate([y_meta.reshape(bsz, N_META, SSM_DIM), y_real], axis=1)
    y = y + d_skip.astype(f32) * uf
    z = jax.nn.gelu(y).astype(u.dtype) @ w_glu
    za, zb = jnp.split(z, 2, axis=-1)
    return za * jax.nn.sigmoid(zb)


def setup_inputs(seed: int = 0) -> dict:
    key = jax.random.key(seed)
    keys = jax.random.split(key, 40)
    counter = [0]
    f32 = jnp.float32

    def nk():
        k = keys[counter[0]]
        counter[0] += 1
        return k

    def nrm(shape, scale):
        return jax.random.normal(nk(), shape, f32) * scale

    beta = DEEPNORM_BETA
    L_ = DEPTH
    x = nrm((BATCH, SEQ, D_MODEL), 1.0)
    meta_tokens = nrm((N_META, D_MODEL), 1.0)
    ln0_g = 1.0 + nrm((D_MODEL,), 0.02)
    ln0_b = nrm((D_MODEL,), 0.02)
    w_in = nrm((L_, D_MODEL, N_IN), D_MODEL ** -0.5)
    b_gate = nrm((L_, 3 * D_MODEL), 0.01)
    kv_norm_g = 1.0 + nrm((L_, KV_RANK), 0.02)
    w_ukv = nrm((L_, KV_RANK, MLA_HEADS * (QK_NOPE + V_DIM)), KV_RANK ** -0.5)
    w_mla_out = nrm((L_, MLA_HEADS * V_DIM, D_MODEL), beta * (MLA_HEADS * V_DIM) ** -0.5)
    conv_w = nrm((L_, CONV_K, CONV_DIM), CONV_K ** -0.5)
    conv_b = nrm((L_, CONV_DIM), 0.01)
    conv_ln_g = 1.0 + nrm((L_, CONV_DIM), 0.02)
    conv_ln_b = nrm((L_, CONV_DIM), 0.02)
    w_conv_out = nrm((L_, CONV_DIM, D_MODEL), beta * CONV_DIM ** -0.5)
    n_idx = jnp.arange(SSM_STATE, dtype=f32)
    ssm_lam_re = -0.5 + nrm((L_, SSM_GROUPS, SSM_STATE), 0.01)
    ssm_lam_im = math.pi * n_idx + nrm((L_, SSM_GROUPS, SSM_STATE), 0.01)
    ssm_log_dt = jax.random.uniform(nk(), (L_, SSM_GROUPS), f32, math.log(1e-3), math.log(1e-1))
    ssm_b_re = nrm((L_, SSM_GROUPS, SSM_STATE, SSM_GROUP), (2 * SSM_GROUP) ** -0.5)
    ssm_b_im = nrm((L_, SSM_GROUPS, SSM_STATE, SSM_GROUP), (2 * SSM_GROUP) ** -0.5)
    ssm_c_re = nrm((L_, SSM_GROUPS, SSM_GROUP, SSM_STATE), (2 * SSM_STATE) ** -0.5)
    ssm_c_im = nrm((L_, SSM_GROUPS, SSM_GROUP, SSM_STATE), (2 * SSM_STATE) ** -0.5)
    ssm_d = nrm((L_, SSM_DIM), 1.0)
    w_glu = nrm((L_, SSM_DIM, 2 * D_MODEL), beta * SSM_DIM ** -0.5)
    w_out = nrm((L_, D_MODEL, D_MODEL), beta * D_MODEL ** -0.5)
    ln1_g = 1.0 + nrm((L_, D_MODEL), 0.02)
    ln1_b = nrm((L_, D_MODEL), 0.02)
    w_ffn_gate = nrm((L_, D_MODEL, FFN_DIM), D_MODEL ** -0.5)
    w_ffn_up = nrm((L_, D_MODEL, FFN_DIM), beta * D_MODEL ** -0.5)
    w_ffn_down = nrm((L_, FFN_DIM, D_MODEL), beta * FFN_DIM ** -0.5)
    ln2_g = 1.0 + nrm((L_, D_MODEL), 0.02)
    ln2_b = nrm((L_, D_MODEL), 0.02)
    return {
        'x': x, 'meta_tokens': meta_tokens, 'ln0_g': ln0_g, 'ln0_b': ln0_b,
        'w_in': w_in, 'b_gate': b_gate, 'kv_norm_g': kv_norm_g, 'w_ukv': w_ukv,
        'w_mla_out': w_mla_out, 'conv_w': conv_w, 'conv_b': conv_b,
        'conv_ln_g': conv_ln_g, 'conv_ln_b': conv_ln_b, 'w_conv_out': w_conv_out,
        'ssm_lam_re': ssm_lam_re, 'ssm_lam_im': ssm_lam_im, 'ssm_log_dt': ssm_log_dt,
        'ssm_b_re': ssm_b_re, 'ssm_b_im': ssm_b_im, 'ssm_c_re': ssm_c_re, 'ssm_c_im': ssm_c_im,
        'ssm_d': ssm_d, 'w_glu': w_glu, 'w_out': w_out, 'ln1_g': ln1_g, 'ln1_b': ln1_b,
        'w_ffn_gate': w_ffn_gate, 'w_ffn_up': w_ffn_up, 'w_ffn_down': w_ffn_down,
        'ln2_g': ln2_g, 'ln2_b': ln2_b,
    }


def reference(x, meta_tokens, ln0_g, ln0_b, w_in, b_gate, kv_norm_g, w_ukv, w_mla_out,
              conv_w, conv_b, conv_ln_g, conv_ln_b, w_conv_out, ssm_lam_re, ssm_lam_im,
              ssm_log_dt, ssm_b_re, ssm_b_im, ssm_c_re, ssm_c_im, ssm_d, w_glu, w_out,
              ln1_g, ln1_b, w_ffn_gate, w_ffn_up, w_ffn_down, ln2_g, ln2_b):
    bsz = x.shape[0]
    meta = jnp.broadcast_to(meta_tokens[None].astype(x.dtype), (bsz, N_META, x.shape[-1]))
    h = _layer_norm(jnp.concatenate([meta, x], axis=1), ln0_g, ln0_b)
    length = h.shape[1]
    cos, sin = _rope_tables(length)
    cid = _chunk_ids(length)
    o = IN_OFFSETS
    for l in range(DEPTH):
        p = h @ w_in[l]
        q = p[..., o[0]:o[1]]
        c_kv = p[..., o[1]:o[2]]
        k_rope = p[..., o[2]:o[3]]
        conv_val = p[..., o[3]:o[4]]
        conv_gate = p[..., o[4]:o[5]]
        u = p[..., o[5]:o[6]]
        gates = jax.nn.sigmoid(p[..., o[6]:o[9]] + b_gate[l])
        g_a = gates[..., :D_MODEL]
        g_b = gates[..., D_MODEL:2 * D_MODEL]
        g_c = gates[..., 2 * D_MODEL:]
        y_a = _mla_branch(q, c_kv, k_rope, kv_norm_g[l], w_ukv[l], w_mla_out[l], cos, sin, cid)
        y_b = _conv_branch(conv_val, conv_gate, conv_w[l], conv_b[l], conv_ln_g[l],
                           conv_ln_b[l], w_conv_out[l])
        y_c = _ssm_branch(u, ssm_lam_re[l], ssm_lam_im[l], ssm_log_dt[l], ssm_b_re[l],
                          ssm_b_im[l], ssm_c_re[l], ssm_c_im[l], ssm_d[l], w_glu[l])
        mixed = (g_a * y_a + g_b * y_b + g_c * y_c) @ w_out[l]
        h = _layer_norm(DEEPNORM_ALPHA * h + mixed, ln1_g[l], ln1_b[l])
        f = (jax.nn.silu(h @ w_ffn_gate[l]) * (h @ w_ffn_up[l])) @ w_ffn_down[l]
        h = _layer_norm(DEEPNORM_ALPHA * h + f, ln2_g[l], ln2_b[l])
    return h[:, N_META:]
```

```python
import contextlib
import math

import numpy as np
import ml_dtypes

import concourse.bass as bass
import concourse.mybir as mybir
from concourse.bass_utils import run_bass_kernel_spmd

F32 = mybir.dt.float32
BF16 = mybir.dt.bfloat16
ALU = mybir.AluOpType
AF = mybir.ActivationFunctionType

D = 2048
NCH = D // 128
H = 16
NOPE = 128
ROPE = 64
VD = 128
RANK = 512
FFN = 5632
FCH = FFN // 128
N_IN = 15936
N_META = 16
CONV_K = 31
SSM_P = 64
NPAIR = 64
ALPHA = (2 * 4) ** 0.25
MLA_SCALE = (NOPE + ROPE) ** -0.5
TT = 512

O_QN = 0
O_QR1 = 2048
O_QR2 = 2560
O_CKV = 3072
O_KR = 3584
O_CV = 3648
O_CG = 5696
O_U = 7744
O_G = 9792

V_LN1G, V_LN1B, V_LN2G, V_LN2B, V_CB, V_CLG, V_CLB, V_SD = [16 * i for i in range(8)]
V_BG = 128
V_KVG = 176
NV = 180


I32 = mybir.dt.int32
TWO_PI = 2.0 * math.pi
CW1 = 6.28125
CW2 = TWO_PI - CW1


def emit_sin(k, dst, dstb, a, ab, t, tb, ti, tib, extra_reads=()):
    dve = k.dve
    k.op(dve, lambda h: h.tensor_scalar(out=t, in0=a, scalar1=1.0 / TWO_PI, scalar2=None, op0=ALU.mult),
         reads=[ab] + list(extra_reads), writes=[tb])
    k.op(dve, lambda h: h.tensor_copy(out=ti, in_=t), reads=[tb], writes=[tib])
    k.op(dve, lambda h: h.tensor_copy(out=t, in_=ti), reads=[tib], writes=[tb])
    k.op(dve, lambda h: h.scalar_tensor_tensor(out=a, in0=t, scalar=-CW1, in1=a, op0=ALU.mult, op1=ALU.add),
         reads=[tb], writes=[ab])
    k.op(dve, lambda h: h.scalar_tensor_tensor(out=a, in0=t, scalar=-CW2, in1=a, op0=ALU.mult, op1=ALU.add),
         reads=[tb], writes=[ab])
    k.op(dve, lambda h: h.tensor_scalar(out=a, in0=a, scalar1=-math.pi, scalar2=math.pi, op0=ALU.max, op1=ALU.min),
         reads=[], writes=[ab])
    k.op(k.act, lambda h: h.activation(out=dst, in_=a, func=AF.Sin), reads=[ab], writes=[dstb])


class Buf:
    __slots__ = ("w", "r")

    def __init__(self):
        self.w = None
        self.r = {}


class Eng:
    def __init__(self, k, name, h):
        self.k = k
        self.name = name
        self.h = h
        self.sem = k.new_sem("e_" + name)
        self.cnt = 0
        self.seen = {}

    def wait(self, ev):
        if ev is None:
            return
        sem, val = ev
        if val <= 0:
            return
        if sem is self.sem and self.name == "pe":
            return
        key = id(sem)
        if self.seen.get(key, 0) >= val:
            return
        self.h.wait_ge(sem, val)
        self.seen[key] = val

    def done(self, ins):
        ins.then_inc(self.sem, 1)
        self.cnt += 1
        return (self.sem, self.cnt)

    def future(self):
        return (self.sem, self.cnt + 1)


class DmaQ:
    def __init__(self, k, name, eng, nsem):
        self.eng = eng
        self.sems = [k.new_sem(f"d_{name}{i}") for i in range(nsem)]
        self.cnt = [0] * nsem
        self.n = 0


class K:
    def __init__(self, nc):
        self.nc = nc
        self.es = contextlib.ExitStack()
        self.nsem = 0
        self.pe = Eng(self, "pe", nc.tensor)
        self.act = Eng(self, "act", nc.scalar)
        self.dve = Eng(self, "dve", nc.vector)
        self.pool = Eng(self, "pool", nc.gpsimd)
        self.sp = Eng(self, "sp", nc.sync)
        self.engs = [self.pe, self.act, self.dve, self.pool, self.sp]
        self.qs = DmaQ(self, "s", self.sp, 12)
        self.qg = DmaQ(self, "g", self.pool, 12)
        self.qa = DmaQ(self, "a", self.act, 8)
        self.queues = [self.qs, self.qg, self.qa]
        self.uid = 0

    def new_sem(self, name):
        self.nsem += 1
        return self.es.enter_context(self.nc.semaphore(name))

    def _pre(self, eng, reads, writes):
        for b in reads:
            eng.wait(b.w)
        for b in writes:
            eng.wait(b.w)
            for e in b.r.values():
                eng.wait(e)

    def _post(self, ev, reads, writes):
        sid = id(ev[0])
        for b in reads:
            b.r[sid] = ev
        for b in writes:
            b.w = ev
            b.r = {}

    def op(self, eng, emit, reads=(), writes=()):
        self._pre(eng, reads, writes)
        ins = emit(eng.h)
        ev = eng.done(ins)
        self._post(ev, reads, writes)
        return ev

    def mm(self, out, lhsT, rhs, start, stop, reads=(), writes=(), inc=True):
        pe = self.pe
        self._pre(pe, reads, writes)
        ins = pe.h.matmul(out, lhsT=lhsT, rhs=rhs, start=start, stop=stop, skip_group_check=True)
        if inc:
            ev = pe.done(ins)
        else:
            ev = pe.future()
        self._post(ev, reads, writes)
        return ev

    def dma(self, q, out, in_, reads=(), writes=()):
        eng = q.eng
        j = q.n % len(q.sems)
        q.n += 1
        sem = q.sems[j]
        eng.wait((sem, q.cnt[j]))
        self._pre(eng, reads, writes)
        eng.h.dma_start(out=out, in_=in_).then_inc(sem, 16)
        q.cnt[j] += 16
        ev = (sem, q.cnt[j])
        self._post(ev, reads, writes)
        return ev

    def barrier(self):
        evs = [(e.sem, e.cnt) for e in self.engs if e.cnt > 0]
        for q in self.queues:
            for s, c in zip(q.sems, q.cnt):
                if c > 0:
                    evs.append((s, c))
        for e in self.engs:
            for ev in evs:
                e.wait(ev)

    def name(self, p):
        self.uid += 1
        return f"{p}{self.uid}"


class Phase:
    def __init__(self, k):
        self.k = k
        self.es = contextlib.ExitStack()

    def sb(self, shape, dt, nm="t"):
        t = self.es.enter_context(self.k.nc.sbuf_tensor(self.k.name(nm), list(shape), dt))
        return t

    def ps(self, nm="ps"):
        return self.es.enter_context(self.k.nc.psum_tensor(self.k.name(nm), [128, 512], F32))

    def ring(self, n, shape, dt, nm="r"):
        return Ring([(self.sb(shape, dt, nm), Buf()) for _ in range(n)])

    def psring(self, n):
        return Ring([(self.ps(), Buf()) for _ in range(n)])

    def close(self):
        self.k.barrier()
        self.es.close()


class Ring:
    def __init__(self, items):
        self.items = items
        self.i = 0

    def next(self):
        it = self.items[self.i % len(self.items)]
        self.i += 1
        return it


def make_tiles(L):
    tiles = [(0, N_META)]
    t = N_META
    while t < L:
        tiles.append((t, min(TT, L - t)))
        t += TT
    return tiles


def supertiles(tiles, n):
    return [tiles[i:i + n] for i in range(0, len(tiles), n)]


class Prog:
    def __init__(self, L, depth, debug=False, tiny=False):
        self.tiny = tiny
        self.L = L
        self.depth = depth
        self.debug = debug
        self.tiles = make_tiles(L)
        self.nc = bass.Bass("TRN2", target_bir_lowering=False)
        self.k = K(self.nc)
        self.dbg_outs = []
        self.declare()
        self._consts = {}
        self.eps_ap(1e-5)
        self.eps_ap(1e-6)

    def eps_ap(self, val):
        if val not in self._consts:
            t = self.k.es.enter_context(self.nc.sbuf_tensor(self.k.name("const"), [128, 1], F32))
            self.k.op(self.k.pool, lambda h: h.memset(t[:], float(val)))
            self.k.barrier()
            self._consts[val] = t
        return self._consts[val][:, 0:1]

    def din(self, name, shape, dt=F32):
        return self.nc.dram_tensor(name, list(shape), dt, kind="ExternalInput").ap()

    def dscr(self, name, shape, dt):
        kind = "ExternalOutput" if (self.debug and name in DEBUG_NAMES) else "Internal"
        if kind == "ExternalOutput":
            self.dbg_outs.append(name)
        return self.nc.dram_tensor(name, list(shape), dt, kind=kind).ap()

    def declare(self):
        L, dep = self.L, self.depth
        d0 = self.din
        tiny = self.tiny

        def d(name, shape, dt=F32):
            if tiny and shape[0] >= 512 and name not in ("xT",):
                shape = [128] + list(shape[1:])
            return d0(name, shape, dt)

        self.xT = d("xT", [D, L])
        self.ln0 = d("ln0", [128, 32])
        self.w_in = d("w_in", [dep * D, N_IN])
        self.w_ukv = d("w_ukv", [dep * RANK, 4096])
        self.w_mla = d("w_mla_out", [dep * D, D])
        self.w_conv = d("w_conv_out", [dep * D, D])
        self.w_glu = d("w_glu", [dep * D, 2 * D])
        self.w_out = d("w_out", [dep * D, D])
        self.w_fg = d("w_ffn_gate", [dep * D, FFN])
        self.w_fu = d("w_ffn_up", [dep * D, FFN])
        self.w_fd = d("w_ffn_down", [dep * FFN, D])
        self.vecs = d("vecs", [dep * 128, NV])
        self.convw = d("convw", [dep * 128, NCH * CONV_K])
        self.lam = d("lam", [dep * 128, 3 * NPAIR])
        self.bT = d("bT", [dep * 2 * NPAIR * 128, 128])
        self.cT = d("cT", [dep * 2 * 128, NPAIR * 128])
        self.cosr = d("cosr", [128, L])
        self.sinr = d("sinr", [128, L])
        self.masks = d("masks", [4 * 128, TT], BF16)
        self.iota1 = d("iota1", [128, TT])
        s0 = self.dscr

        def s(name, shape, dt):
            if tiny and name.startswith("w") or tiny and name == "bTb":
                shape = [128] + list(shape[1:])
            return s0(name, shape, dt)
        self.winb = s("winb", [dep * D, N_IN], BF16)
        self.wukvb = s("wukvb", [dep * RANK, 4096], BF16)
        self.wmlab = s("wmlab", [dep * D, D], BF16)
        self.wconvb = s("wconvb", [dep * D, D], BF16)
        self.wglub = s("wglub", [dep * D, 2 * D], BF16)
        self.woutb = s("woutb", [dep * D, D], BF16)
        self.wfgb = s("wfgb", [dep * D, FFN], BF16)
        self.wfub = s("wfub", [dep * D, FFN], BF16)
        self.wfdb = s("wfdb", [dep * FFN, D], BF16)
        self.bTb = s("bTb", [dep * 2 * NPAIR * 128, 128], BF16)
        self.hT = s("hT", [D, L], F32)
        self.hb = s("hb", [D, L], BF16)
        self.qn = s("qn", [D, L], BF16)
        self.qr1 = s("qr1", [512, L], BF16)
        self.qr2 = s("qr2", [512, L], BF16)
        self.ckvn = s("ckvn", [RANK, L], BF16)
        self.kr = s("kr", [ROPE, L], BF16)
        self.zT = s("zT", [D, L], F32)
        self.uT = s("uT", [D, L], F32)
        self.gates = s("gates", [3 * D, L], F32)
        self.knT = s("knT", [D, L], BF16)
        self.vtok = s("vtok", [L, D], BF16)
        self.oT = s("oT", [D, L], BF16)
        self.sT = s("sT", [D, L], BF16)
        self.gsm = s("gsm", [D, L], BF16)
        self.mix = s("mix", [D, L], F32)
        self.mixb = s("mixb", [D, L], BF16)
        self.pre = s("pre", [D, L], F32)
        self.aT = s("aT", [FFN, L], BF16)
        self.outT = self.nc.dram_tensor("outT", [D, L - N_META], F32, kind="ExternalOutput").ap()

    def phase_cast(self):
        k = self.k
        q = k.qg

        def cast2d(dst, src, rows, cols, cstep=3984):
            for r0 in range(0, rows, 128):
                rn = min(128, rows - r0)
                for c0 in range(0, cols, cstep):
                    cn = min(cstep, cols - c0)
                    k.dma(q, dst[r0:r0 + rn, c0:c0 + cn], src[r0:r0 + rn, c0:c0 + cn])

        dep = self.depth
        for r0 in range(0, dep * D, 128):
            src = self.w_in[r0:r0 + 128, :]
            dst = self.winb[r0:r0 + 128, :]
            sq = src[:, 0:3072].rearrange("r (h e) -> r h e", e=192)
            k.dma(q, dst[:, O_QN:O_QN + 2048].rearrange("r (h e) -> r h e", e=128), sq[:, :, 0:128])
            k.dma(q, dst[:, O_QR1:O_QR1 + 512].rearrange("r (h e) -> r h e", e=32), sq[:, :, 128:160])
            k.dma(q, dst[:, O_QR2:O_QR2 + 512].rearrange("r (h e) -> r h e", e=32), sq[:, :, 160:192])
            for c0 in range(3072, N_IN, 3216):
                cn = min(3216, N_IN - c0)
                k.dma(q, dst[:, c0:c0 + cn], src[:, c0:c0 + cn])
        for r0 in range(0, dep * RANK, 128):
            src = self.w_ukv[r0:r0 + 128, :].rearrange("r (h two e) -> r h two e", two=2, e=128)
            dst = self.wukvb[r0:r0 + 128, :]
            k.dma(q, dst[:, 0:2048].rearrange("r (h e) -> r h e", e=128), src[:, :, 0, :])
            k.dma(q, dst[:, 2048:4096].rearrange("r (h e) -> r h e", e=128), src[:, :, 1, :])
        cast2d(self.wmlab, self.w_mla, dep * D, D, 2048)
        cast2d(self.wconvb, self.w_conv, dep * D, D, 2048)
        cast2d(self.wglub, self.w_glu, dep * D, 2 * D, 4096)
        cast2d(self.woutb, self.w_out, dep * D, D, 2048)
        cast2d(self.wfgb, self.w_fg, dep * D, FFN, 2816)
        cast2d(self.wfub, self.w_fu, dep * D, FFN, 2816)
        cast2d(self.wfdb, self.w_fd, dep * FFN, D, 2048)
        bsrc = self.bT.rearrange("(a p) s -> p a s", p=128)
        bdst = self.bTb.rearrange("(a p) s -> p a s", p=128)
        na = dep * 2 * NPAIR
        for a0 in range(0, na, 32):
            k.dma(q, bdst[:, a0:a0 + 32, :], bsrc[:, a0:a0 + 32, :])
        k.barrier()

    def phase_ln(self, src, gsrc, gcol, bcol, dst_f32=None, dst_bf=None, out_final=None):
        k = self.k
        ph = Phase(k)
        nv = gsrc.shape[1]
        gv = ph.sb([128, nv], F32, "gv")
        gvb = Buf()
        k.dma(k.qs, gv[:], gsrc, writes=[gvb])
        ones = ph.sb([128, 128], F32, "ones")
        onesb = Buf()
        k.op(k.dve, lambda h: h.memset(ones[:], 1.0), writes=[onesb])
        xr = ph.ring(2, [128, NCH, TT], F32, "x")
        sqr = ph.ring(3, [128, TT], F32, "sq")
        psr = ph.psring(4)
        st = ph.ring(2, [128, 3, TT], F32, "st")
        tr = ph.ring(3, [128, TT], F32, "t")
        of = ph.ring(3, [128, TT], F32, "of")
        ob = ph.ring(3, [128, TT], BF16, "ob")
        for (t0, tn) in self.tiles:
            x, xb = xr.next()
            k.dma(k.qs, x[:, :, 0:tn], src.rearrange("(c p) t -> p c t", p=128)[:, :, t0:t0 + tn], writes=[xb])
            s_ps, s_b = psr.next()
            q_ps, q_b = psr.next()
            for c in range(NCH):
                k.mm(s_ps[:, 0:tn], ones[:], x[:, c, 0:tn], c == 0, c == NCH - 1, reads=[onesb, xb],
                     writes=[s_b], inc=(c == NCH - 1))
            for c in range(NCH):
                sq, sqb = sqr.next()
                k.op(k.act, lambda h: h.activation(out=sq[:, 0:tn], in_=x[:, c, 0:tn], func=AF.Square),
                     reads=[xb], writes=[sqb])
                k.mm(q_ps[:, 0:tn], ones[:], sq[:, 0:tn], c == 0, c == NCH - 1, reads=[onesb, sqb],
                     writes=[q_b], inc=True)
            s, sb_ = st.next()
            mean, msq, rstd = s[:, 0, 0:tn], s[:, 1, 0:tn], s[:, 2, 0:tn]
            k.op(k.dve, lambda h: h.tensor_scalar(out=mean, in0=s_ps[:, 0:tn], scalar1=1.0 / D, scalar2=None,
                                                  op0=ALU.mult), reads=[s_b], writes=[sb_])
            k.op(k.dve, lambda h: h.tensor_tensor(out=msq, in0=mean, in1=mean, op=ALU.mult),
                 reads=[sb_], writes=[sb_])
            k.op(k.dve, lambda h: h.scalar_tensor_tensor(out=rstd, in0=q_ps[:, 0:tn], scalar=1.0 / D, in1=msq,
                                                         op0=ALU.mult, op1=ALU.subtract),
                 reads=[q_b, sb_], writes=[sb_])
            k.op(k.act, lambda h: h.activation(out=rstd, in_=rstd, func=AF.Sqrt, bias=self.eps_ap(1e-5), scale=1.0),
                 reads=[sb_], writes=[sb_])
            k.op(k.dve, lambda h: h.reciprocal(out=rstd, in_=rstd), reads=[sb_], writes=[sb_])
            for c in range(NCH):
                t, tb = tr.next()
                k.op(k.dve, lambda h: h.tensor_tensor(out=t[:, 0:tn], in0=x[:, c, 0:tn], in1=mean, op=ALU.subtract),
                     reads=[xb, sb_], writes=[tb])
                k.op(k.dve, lambda h: h.tensor_tensor(out=t[:, 0:tn], in0=t[:, 0:tn], in1=rstd, op=ALU.mult),
                     reads=[sb_, tb], writes=[tb])
                o, o_b = of.next()
                k.op(k.act, lambda h: h.activation(out=o[:, 0:tn], in_=t[:, 0:tn], func=AF.Identity,
                                                   bias=gv[:, bcol + c:bcol + c + 1],
                                                   scale=gv[:, gcol + c:gcol + c + 1]),
                     reads=[tb, gvb], writes=[o_b])
                if dst_f32 is not None:
                    k.dma(k.qg, dst_f32[c * 128:(c + 1) * 128, t0:t0 + tn], o[:, 0:tn], reads=[o_b])
                if out_final is not None and t0 >= N_META:
                    k.dma(k.qg, out_final[c * 128:(c + 1) * 128, t0 - N_META:t0 - N_META + tn], o[:, 0:tn],
                          reads=[o_b])
                if dst_bf is not None:
                    o2, o2b = ob.next()
                    k.op(k.pool, lambda h: h.tensor_copy(out=o2[:, 0:tn], in_=o[:, 0:tn]), reads=[o_b], writes=[o2b])
                    k.dma(k.qg, dst_bf[c * 128:(c + 1) * 128, t0:t0 + tn], o2[:, 0:tn], reads=[o2b])
        ph.close()

    def phase_linear(self, src, KC, wb, groups, st_n, setup=None, wcols=512):
        k = self.k
        ph = Phase(k)
        ctx = {"ph": ph}
        if setup is not None:
            setup(ctx)
        groups = [([(c if len(c) == 3 else (0, c[0], c[1])) for c in g[0]], g[1]) for g in groups]
        blocks = []
        cur, ncur = [], 0
        for g in groups:
            n = sum(c[-1] for c in g[0])
            if cur and ncur + n > wcols:
                blocks.append(cur)
                cur, ncur = [], 0
            cur.append(g)
            ncur += n
        if cur:
            blocks.append(cur)
        sts = supertiles(self.tiles, st_n)
        max_tok = max(sum(t[1] for t in st) for st in sts)
        ar = ph.ring(2, [128, KC, max_tok], BF16, "act")
        wr = ph.ring(2, [128, KC, wcols], BF16, "w")
        psr = ph.psring(8)
        ctx["psr"] = psr
        srcv = src.rearrange("(c p) t -> p c t", p=128)
        wbs = wb if isinstance(wb, (list, tuple)) else [wb]
        wvs = [x.rearrange("(c p) n -> p c n", p=128) for x in wbs]
        for st in sts:
            a, ab = ar.next()
            s0 = st[0][0]
            stok = sum(t[1] for t in st)
            for c0 in range(0, KC, 8):
                c1 = min(KC, c0 + 8)
                k.dma(k.qs, a[:, c0:c1, 0:stok], srcv[:, c0:c1, s0:s0 + stok], writes=[ab])
            ctx["st"] = st
            if "st_begin" in ctx:
                ctx["st_begin"](ctx, st)
            for blk in blocks:
                w, wbuf = wr.next()
                ranges = []
                off = 0
                for g in blk:
                    for (wi, c0, n) in g[0]:
                        if ranges and ranges[-1][3] == wi and ranges[-1][0] + ranges[-1][1] == c0:
                            ranges[-1][1] += n
                        else:
                            ranges.append([c0, n, off, wi])
                        off += n
                for (c0, n, o, wi) in ranges:
                    for kc0 in range(0, KC, 16):
                        kc1 = min(KC, kc0 + 16)
                        k.dma(k.qs, w[:, kc0:kc1, o:o + n], wvs[wi][:, kc0:kc1, c0:c0 + n], writes=[wbuf])
                off = 0
                for g in blk:
                    offs = []
                    for (wi, c0, n) in g[0]:
                        offs.append((off, n))
                        off += n
                    for (t0, tn) in st:
                        a0 = t0 - s0
                        pss = []
                        for (o, n) in offs:
                            ps, psb = psr.next()
                            for c in range(KC):
                                k.mm(ps[0:n, 0:tn], w[:, c, o:o + n], a[:, c, a0:a0 + tn], c == 0, c == KC - 1,
                                     reads=[wbuf, ab], writes=[psb], inc=(c == KC - 1))
                            pss.append((ps, psb))
                        g[1](ctx, t0, tn, pss)
        ph.close()

    def phase_inproj(self, l):
        k = self.k
        L = self.L
        P = self

        def setup(ctx):
            ph = ctx["ph"]
            vec = ph.sb([128, NV], F32, "vec")
            vb = Buf()
            k.dma(k.qs, vec[:], self.vecs[l * 128:(l + 1) * 128, :], writes=[vb])
            ctx["vec"], ctx["vb"] = vec, vb
            ones = ph.sb([128, 128], F32, "ones")
            ob = Buf()
            k.op(k.dve, lambda h: h.memset(ones[:], 1.0), writes=[ob])
            ctx["ones"], ctx["onesb"] = ones, ob
            ctx["cs"] = ph.ring(1, [128, 3 * TT], F32, "cs")
            ctx["sn"] = ph.ring(1, [128, 3 * TT], F32, "sn")
            ctx["obf"] = ph.ring(4, [128, TT], BF16, "obf")
            ctx["of"] = ph.ring(4, [128, TT], F32, "of")
            ctx["tmp"] = ph.ring(6, [128, TT], F32, "tmp")

            def st_begin(ctx, st):
                s0 = st[0][0]
                stok = sum(t[1] for t in st)
                cs, csb = ctx["cs"].next()
                sn, snb = ctx["sn"].next()
                k.dma(k.qs, cs[:, 0:stok], self.cosr[:, s0:s0 + stok], writes=[csb])
                k.dma(k.qs, sn[:, 0:stok], self.sinr[:, s0:s0 + stok], writes=[snb])
                ctx["cur"] = (cs, csb, sn, snb, s0)

            ctx["st_begin"] = st_begin

        def epi_qn(hh):
            def f(ctx, t0, tn, pss):
                (ps, psb), = pss
                o, ob = ctx["obf"].next()
                k.op(k.act, lambda h: h.activation(out=o[:, 0:tn], in_=ps[:, 0:tn], func=AF.Copy, scale=MLA_SCALE),
                     reads=[psb], writes=[ob])
                k.dma(k.qg, P.qn[hh * 128:(hh + 1) * 128, t0:t0 + tn], o[:, 0:tn], reads=[ob])
            return f

        def rope(ctx, t0, tn, p1, p1b, p2, p2b, np_, scale, dst1, dst2):
            cs, csb, sn, snb, s0 = ctx["cur"]
            a0 = t0 - s0
            c = cs[0:np_, a0:a0 + tn]
            s = sn[0:np_, a0:a0 + tn]
            t1, t1b = ctx["tmp"].next()
            t2, t2b = ctx["tmp"].next()
            t3, t3b = ctx["tmp"].next()
            t4, t4b = ctx["tmp"].next()
            for (tt, ttb, pp, ppb, tab, tabb) in ((t1, t1b, p1, p1b, c, csb), (t2, t2b, p2, p2b, s, snb),
                                                  (t3, t3b, p1, p1b, s, snb), (t4, t4b, p2, p2b, c, csb)):
                k.op(k.dve, lambda h: h.scalar_tensor_tensor(out=tt[0:np_, 0:tn], in0=pp[0:np_, 0:tn], scalar=scale,
                                                             in1=tab, op0=ALU.mult, op1=ALU.mult),
                     reads=[ppb, tabb], writes=[ttb])
            o1, o1b = ctx["obf"].next()
            o2, o2b = ctx["obf"].next()
            k.op(k.pool, lambda h: h.tensor_tensor(out=o1[0:np_, 0:tn], in0=t1[0:np_, 0:tn], in1=t2[0:np_, 0:tn],
                                                   op=ALU.subtract), reads=[t1b, t2b], writes=[o1b])
            k.op(k.pool, lambda h: h.tensor_tensor(out=o2[0:np_, 0:tn], in0=t3[0:np_, 0:tn], in1=t4[0:np_, 0:tn],
                                                   op=ALU.add), reads=[t3b, t4b], writes=[o2b])
            k.dma(k.qg, dst1[:, t0:t0 + tn], o1[0:np_, 0:tn], reads=[o1b])
            k.dma(k.qg, dst2[:, t0:t0 + tn], o2[0:np_, 0:tn], reads=[o2b])

        def epi_qr(g):
            def f(ctx, t0, tn, pss):
                (p1, p1b), (p2, p2b) = pss
                rope(ctx, t0, tn, p1, p1b, p2, p2b, 128, MLA_SCALE,
                     P.qr1[g * 128:(g + 1) * 128, :], P.qr2[g * 128:(g + 1) * 128, :])
            return f

        def epi_kr(ctx, t0, tn, pss):
            (p1, p1b), (p2, p2b) = pss
            rope(ctx, t0, tn, p1, p1b, p2, p2b, 32, 1.0, P.kr[0:32, :], P.kr[32:64, :])

        def epi_ckv(ctx, t0, tn, pss):
            vec, vb = ctx["vec"], ctx["vb"]
            ones, onesb = ctx["ones"], ctx["onesb"]
            sps, spsb = ctx["psr"].next()
            for i, (ps, psb) in enumerate(pss):
                sq, sqb = ctx["tmp"].next()
                k.op(k.act, lambda h: h.activation(out=sq[:, 0:tn], in_=ps[:, 0:tn], func=AF.Square),
                     reads=[psb], writes=[sqb])
                k.mm(sps[:, 0:tn], ones[:], sq[:, 0:tn], i == 0, i == 3, reads=[onesb, sqb], writes=[spsb], inc=True)
            r, rb = ctx["tmp"].next()
            k.op(k.act, lambda h: h.activation(out=r[:, 0:tn], in_=sps[:, 0:tn], func=AF.Sqrt, bias=P.eps_ap(1e-6),
                                               scale=1.0 / RANK), reads=[spsb], writes=[rb])
            k.op(k.dve, lambda h: h.reciprocal(out=r[:, 0:tn], in_=r[:, 0:tn]), reads=[rb], writes=[rb])
            for i, (ps, psb) in enumerate(pss):
                o, ob = ctx["obf"].next()
                k.op(k.dve, lambda h: h.scalar_tensor_tensor(out=o[:, 0:tn], in0=ps[:, 0:tn],
                                                             scalar=vec[:, V_KVG + i:V_KVG + i + 1], in1=r[:, 0:tn],
                                                             op0=ALU.mult, op1=ALU.mult),
                     reads=[psb, vb, rb], writes=[ob])
                k.dma(k.qg, P.ckvn[i * 128:(i + 1) * 128, t0:t0 + tn], o[:, 0:tn], reads=[ob])

        def epi_conv(i):
            def f(ctx, t0, tn, pss):
                (pv, pvb), (pg, pgb) = pss
                sg, sgb = ctx["tmp"].next()
                k.op(k.act, lambda h: h.activation(out=sg[:, 0:tn], in_=pg[:, 0:tn], func=AF.Sigmoid),
                     reads=[pgb], writes=[sgb])
                o, ob = ctx["of"].next()
                k.op(k.dve, lambda h: h.tensor_tensor(out=o[:, 0:tn], in0=pv[:, 0:tn], in1=sg[:, 0:tn], op=ALU.mult),
                     reads=[pvb, sgb], writes=[ob])
                k.dma(k.qg, P.zT[i * 128:(i + 1) * 128, t0:t0 + tn], o[:, 0:tn], reads=[ob])
            return f

        def epi_u(i):
            def f(ctx, t0, tn, pss):
                (ps, psb), = pss
                o, ob = ctx["of"].next()
                k.op(k.act, lambda h: h.activation(out=o[:, 0:tn], in_=ps[:, 0:tn], func=AF.Copy),
                     reads=[psb], writes=[ob])
                k.dma(k.qg, P.uT[i * 128:(i + 1) * 128, t0:t0 + tn], o[:, 0:tn], reads=[ob])
            return f

        def epi_gate(j):
            def f(ctx, t0, tn, pss):
                (ps, psb), = pss
                vec, vb = ctx["vec"], ctx["vb"]
                o, ob = ctx["of"].next()
                k.op(k.act, lambda h: h.activation(out=o[:, 0:tn], in_=ps[:, 0:tn], func=AF.Sigmoid,
                                                   bias=vec[:, V_BG + j:V_BG + j + 1]),
                     reads=[psb, vb], writes=[ob])
                k.dma(k.qg, P.gates[j * 128:(j + 1) * 128, t0:t0 + tn], o[:, 0:tn], reads=[ob])
            return f

        groups = []
        for hh in range(H):
            groups.append(([(O_QN + hh * 128, 128)], epi_qn(hh)))
        for g in range(4):
            groups.append(([(O_QR1 + g * 128, 128), (O_QR2 + g * 128, 128)], epi_qr(g)))
        groups.append(([(O_CKV + i * 128, 128) for i in range(4)], epi_ckv))
        groups.append(([(O_KR, 32), (O_KR + 32, 32)], epi_kr))
        for i in range(NCH):
            groups.append(([(O_CV + i * 128, 128), (O_CG + i * 128, 128)], epi_conv(i)))
        for i in range(NCH):
            groups.append(([(O_U + i * 128, 128)], epi_u(i)))
        for j in range(3 * NCH):
            groups.append(([(O_G + j * 128, 128)], epi_gate(j)))
        self.phase_linear(self.hb, NCH, self.winb[l * D:(l + 1) * D, :], groups, 3, setup=setup)


DEBUG_NAMES = set()


def phase_kvup(self, l):
    k = self.k
    ph = Phase(k)
    w = ph.sb([128, 4, 4096], BF16, "wukv")
    wb = Buf()
    wv = self.wukvb[l * RANK:(l + 1) * RANK, :].rearrange("(c p) n -> p c n", p=128)
    for c in range(4):
        k.dma(k.qs, w[:, c, :], wv[:, c, :], writes=[wb])
    ar = ph.ring(2, [128, 4, TT], BF16, "ckv")
    psr = ph.psring(6)
    obr = ph.ring(4, [128, TT], BF16, "ob")
    srcv = self.ckvn.rearrange("(c p) t -> p c t", p=128)
    for (t0, tn) in self.tiles:
        a, ab = ar.next()
        k.dma(k.qs, a[:, :, 0:tn], srcv[:, :, t0:t0 + tn], writes=[ab])
        for hh in range(H):
            ps, psb = psr.next()
            for c in range(4):
                k.mm(ps[:, 0:tn], w[:, c, hh * 128:(hh + 1) * 128], a[:, c, 0:tn], c == 0, c == 3,
                     reads=[wb, ab], writes=[psb], inc=(c == 3))
            o, ob = obr.next()
            eng = k.act if hh % 2 == 0 else k.dve
            if eng is k.act:
                k.op(eng, lambda h: h.activation(out=o[:, 0:tn], in_=ps[:, 0:tn], func=AF.Copy), reads=[psb], writes=[ob])
            else:
                k.op(eng, lambda h: h.tensor_copy(out=o[:, 0:tn], in_=ps[:, 0:tn]), reads=[psb], writes=[ob])
            k.dma(k.qg, self.knT[hh * 128:(hh + 1) * 128, t0:t0 + tn], o[:, 0:tn], reads=[ob])
        for s0 in range(0, tn, 128):
            sn = min(128, tn - s0)
            for g in range(4):
                ps, psb = psr.next()
                for c in range(4):
                    k.mm(ps[0:sn, :], a[:, c, s0:s0 + sn], w[:, c, 2048 + g * 512:2048 + (g + 1) * 512], c == 0, c == 3,
                         reads=[wb, ab], writes=[psb], inc=(c == 3))
                o, ob = obr.next()
                eng = k.act if g % 2 == 0 else k.dve
                if eng is k.act:
                    k.op(eng, lambda h: h.activation(out=o[0:sn, :], in_=ps[0:sn, :], func=AF.Copy), reads=[psb], writes=[ob])
                else:
                    k.op(eng, lambda h: h.tensor_copy(out=o[0:sn, :], in_=ps[0:sn, :]), reads=[psb], writes=[ob])
                k.dma(k.qg, self.vtok[t0 + s0:t0 + s0 + sn, g * 512:(g + 1) * 512], o[0:sn, :], reads=[ob])
    ph.close()


Prog.phase_kvup = phase_kvup


def phase_attn(self):
    k = self.k
    L = self.L
    ph = Phase(k)
    nreal = (L - N_META) // 128
    krt = ph.sb([64, L], BF16, "kr")
    krb = Buf()
    k.dma(k.qs, krt[:], self.kr, writes=[krb])
    msk = ph.sb([128, 4, TT], BF16, "msk")
    mskb = Buf()
    k.dma(k.qs, msk[:], self.masks.rearrange("(j p) t -> p j t", p=128), writes=[mskb])
    ones = ph.sb([128, 128], BF16, "ones")
    onesb = Buf()
    k.op(k.dve, lambda h: h.memset(ones[:], 1.0), writes=[onesb])
    knr = ph.ring(2, [128, L], BF16, "kn")
    vr = ph.ring(2, [128, nreal + 1, 128], BF16, "v")
    qnr = ph.ring(2, [128, TT], BF16, "qn")
    qrr = ph.ring(2, [64, TT], BF16, "qr")
    sr = ph.psring(3)
    orr = ph.psring(2)
    smr = ph.psring(2)
    pr = ph.ring(4, [128, TT], BF16, "p")
    rr = ph.ring(2, [128, TT], F32, "rs")
    obr = ph.ring(2, [128, TT], BF16, "ob")
    for hh in range(H):
        kn, knb = knr.next()
        k.dma(k.qs, kn[:], self.knT[hh * 128:(hh + 1) * 128, :], writes=[knb])
        v, vb = vr.next()
        k.dma(k.qs, v[0:N_META, 0, :], self.vtok[0:N_META, hh * 128:(hh + 1) * 128], writes=[vb])
        vsrc = self.vtok[N_META:L, hh * 128:(hh + 1) * 128].rearrange("(m p) d -> p m d", p=128)
        for m0 in range(0, nreal, 16):
            m1 = min(nreal, m0 + 16)
            k.dma(k.qs, v[:, 1 + m0:1 + m1, :], vsrc[:, m0:m1, :], writes=[vb])
        for ti, (t0, tn) in enumerate(self.tiles):
            qn, qnb = qnr.next()
            qr, qrb = qrr.next()
            k.dma(k.qs, qn[:, 0:tn], self.qn[hh * 128:(hh + 1) * 128, t0:t0 + tn], writes=[qnb])
            k.dma(k.qs, qr[0:32, 0:tn], self.qr1[hh * 32:(hh + 1) * 32, t0:t0 + tn], writes=[qrb])
            k.dma(k.qs, qr[32:64, 0:tn], self.qr2[hh * 32:(hh + 1) * 32, t0:t0 + tn], writes=[qrb])
            kts = [(0, N_META, 0, None)]
            if ti >= 1:
                nfull = (t0 - N_META) // 128
                ndiag = (tn + 127) // 128
                for m in range(nfull + ndiag):
                    kts.append((N_META + m * 128, 128, 1 + m, (m - nfull) if m >= nfull else None))
            ops, opsb = orr.next()
            sps, spsb = smr.next()

            def emit_s(kt):
                c0, ks, vi, mi = kt
                s, sb_ = sr.next()
                k.mm(s[0:ks, 0:tn], kn[:, c0:c0 + ks], qn[:, 0:tn], True, False, reads=[knb, qnb], writes=[sb_], inc=False)
                k.mm(s[0:ks, 0:tn], krt[:, c0:c0 + ks], qr[:, 0:tn], False, True, reads=[krb, qrb], writes=[sb_], inc=True)
                return s, sb_

            nk = len(kts)
            cur = emit_s(kts[0])
            for i, kt in enumerate(kts):
                c0, ks, vi, mi = kt
                s, sb_ = cur
                if i + 1 < nk:
                    cur = emit_s(kts[i + 1])
                p, pb = pr.next()
                k.op(k.act, lambda h: h.activation(out=p[0:ks, 0:tn], in_=s[0:ks, 0:tn], func=AF.Exp),
                     reads=[sb_], writes=[pb])
                if mi is not None:
                    k.op(k.pool, lambda h: h.tensor_tensor(out=p[0:ks, 0:tn], in0=p[0:ks, 0:tn], in1=msk[0:ks, mi, 0:tn],
                                                           op=ALU.mult), reads=[mskb, pb], writes=[pb])
                k.mm(ops[:, 0:tn], v[0:ks, vi, :], p[0:ks, 0:tn], i == 0, i == nk - 1, reads=[vb, pb], writes=[opsb],
                     inc=False)
                k.mm(sps[:, 0:tn], ones[0:ks, :], p[0:ks, 0:tn], i == 0, i == nk - 1, reads=[onesb, pb], writes=[spsb],
                     inc=True)
            r, rb = rr.next()
            k.op(k.dve, lambda h: h.reciprocal(out=r[:, 0:tn], in_=sps[:, 0:tn]), reads=[spsb], writes=[rb])
            o, ob = obr.next()
            k.op(k.dve, lambda h: h.tensor_tensor(out=o[:, 0:tn], in0=ops[:, 0:tn], in1=r[:, 0:tn], op=ALU.mult),
                 reads=[opsb, rb], writes=[ob])
            k.dma(k.qg, self.oT[hh * 128:(hh + 1) * 128, t0:t0 + tn], o[:, 0:tn], reads=[ob])
    ph.close()


Prog.phase_attn = phase_attn


def phase_conv(self, l):
    k = self.k
    ph = Phase(k)
    HALO = CONV_K - 1
    vec = ph.sb([128, NV], F32, "vec")
    vb = Buf()
    k.dma(k.qs, vec[:], self.vecs[l * 128:(l + 1) * 128, :], writes=[vb])
    cw = ph.sb([128, NCH, CONV_K], F32, "cw")
    cwb = Buf()
    k.dma(k.qs, cw[:], self.convw[l * 128:(l + 1) * 128, :].rearrange("p (c k) -> p c k", k=CONV_K), writes=[cwb])
    ones = ph.sb([128, 128], F32, "ones")
    onesb = Buf()
    k.op(k.dve, lambda h: h.memset(ones[:], 1.0), writes=[onesb])
    zr = ph.ring(3, [128, HALO + TT], F32, "z")
    yr = ph.ring(2, [128, NCH, TT], F32, "y")
    a2r = ph.ring(2, [128, TT], F32, "a2")
    a3r = ph.ring(2, [128, TT], F32, "a3")
    sqr = ph.ring(3, [128, TT], F32, "sq")
    psr = ph.psring(4)
    st = ph.ring(2, [128, 3, TT], F32, "st")
    tr = ph.ring(3, [128, TT], F32, "t")
    ob = ph.ring(3, [128, TT], BF16, "ob")
    SPLIT = 23
    for (t0, tn) in self.tiles:
        y, yb = yr.next()
        s_ps, s_b = psr.next()
        q_ps, q_b = psr.next()
        for c in range(NCH):
            z, zb = zr.next()
            h0 = max(0, t0 - HALO)
            nh = t0 - h0
            if nh < HALO:
                k.op(k.pool, lambda h: h.memset(z[:, 0:HALO - nh], 0.0), writes=[zb])
            k.dma(k.qs, z[:, HALO - nh:HALO + tn], self.zT[c * 128:(c + 1) * 128, h0:t0 + tn], writes=[zb])
            ycb = Buf()
            k.op(k.dve, lambda h: h.tensor_scalar(out=y[:, c, 0:tn], in0=z[:, 0:tn], scalar1=cw[:, c, 0:1],
                                                  scalar2=vec[:, V_CB + c:V_CB + c + 1], op0=ALU.mult, op1=ALU.add),
                 reads=[zb, cwb, vb], writes=([ycb, yb] if c == 0 else [ycb]))
            for j in range(1, SPLIT):
                k.op(k.dve, lambda h: h.scalar_tensor_tensor(out=y[:, c, 0:tn], in0=z[:, j:j + tn], scalar=cw[:, c, j:j + 1],
                                                             in1=y[:, c, 0:tn], op0=ALU.mult, op1=ALU.add),
                     reads=[zb, cwb], writes=[ycb])
            a2, a2b = a2r.next()
            k.op(k.pool, lambda h: h.tensor_scalar(out=a2[:, 0:tn], in0=z[:, SPLIT:SPLIT + tn], scalar1=cw[:, c, SPLIT:SPLIT + 1],
                                                   scalar2=None, op0=ALU.mult), reads=[zb, cwb], writes=[a2b])
            for j in range(SPLIT + 1, CONV_K):
                a3, a3b = a3r.next()
                k.op(k.pool, lambda h: h.tensor_scalar(out=a3[:, 0:tn], in0=z[:, j:j + tn], scalar1=cw[:, c, j:j + 1],
                                                       scalar2=None, op0=ALU.mult), reads=[zb, cwb], writes=[a3b])
                k.op(k.pool, lambda h: h.tensor_tensor(out=a2[:, 0:tn], in0=a2[:, 0:tn], in1=a3[:, 0:tn], op=ALU.add),
                     reads=[a3b], writes=[a2b])
            k.op(k.dve, lambda h: h.tensor_tensor(out=y[:, c, 0:tn], in0=y[:, c, 0:tn], in1=a2[:, 0:tn], op=ALU.add),
                 reads=[a2b], writes=[ycb])
            k.mm(s_ps[:, 0:tn], ones[:], y[:, c, 0:tn], c == 0, c == NCH - 1, reads=[onesb, ycb], writes=[s_b], inc=True)
            sq, sqb = sqr.next()
            k.op(k.act, lambda h: h.activation(out=sq[:, 0:tn], in_=y[:, c, 0:tn], func=AF.Square), reads=[ycb], writes=[sqb])
            k.mm(q_ps[:, 0:tn], ones[:], sq[:, 0:tn], c == 0, c == NCH - 1, reads=[onesb, sqb], writes=[q_b], inc=True)
            yb.r.update(ycb.r)
            yb.w = ycb.w
            ycb.r = dict(ycb.r)
        s, sb_ = st.next()
        mean, msq, rstd = s[:, 0, 0:tn], s[:, 1, 0:tn], s[:, 2, 0:tn]
        k.op(k.dve, lambda h: h.tensor_scalar(out=mean, in0=s_ps[:, 0:tn], scalar1=1.0 / D, scalar2=None, op0=ALU.mult),
             reads=[s_b], writes=[sb_])
        k.op(k.dve, lambda h: h.tensor_tensor(out=msq, in0=mean, in1=mean, op=ALU.mult), reads=[sb_], writes=[sb_])
        k.op(k.dve, lambda h: h.scalar_tensor_tensor(out=rstd, in0=q_ps[:, 0:tn], scalar=1.0 / D, in1=msq,
                                                     op0=ALU.mult, op1=ALU.subtract), reads=[q_b, sb_], writes=[sb_])
        k.op(k.act, lambda h: h.activation(out=rstd, in_=rstd, func=AF.Sqrt, bias=self.eps_ap(1e-5), scale=1.0),
             reads=[sb_], writes=[sb_])
        k.op(k.dve, lambda h: h.reciprocal(out=rstd, in_=rstd), reads=[sb_], writes=[sb_])
        for c in range(NCH):
            t, tb = tr.next()
            k.op(k.dve, lambda h: h.tensor_tensor(out=t[:, 0:tn], in0=y[:, c, 0:tn], in1=mean, op=ALU.subtract),
                 reads=[yb, sb_], writes=[tb])
            k.op(k.pool, lambda h: h.tensor_tensor(out=t[:, 0:tn], in0=t[:, 0:tn], in1=rstd, op=ALU.mult),
                 reads=[sb_, tb], writes=[tb])
            o, o_b = ob.next()
            k.op(k.act, lambda h: h.activation(out=o[:, 0:tn], in_=t[:, 0:tn], func=AF.Silu,
                                               bias=vec[:, V_CLB + c:V_CLB + c + 1], scale=vec[:, V_CLG + c:V_CLG + c + 1]),
                 reads=[tb, vb], writes=[o_b])
            k.dma(k.qg, self.sT[c * 128:(c + 1) * 128, t0:t0 + tn], o[:, 0:tn], reads=[o_b])
    ph.close()


Prog.phase_conv = phase_conv


def phase_ssm(self, l):
    k = self.k
    ph = Phase(k)
    PI = math.pi
    vec = ph.sb([128, NV], F32, "vec")
    vb = Buf()
    k.dma(k.qs, vec[:], self.vecs[l * 128:(l + 1) * 128, :], writes=[vb])
    lam = ph.sb([128, 3 * NPAIR], F32, "lam")
    lb = Buf()
    k.dma(k.qs, lam[:], self.lam[l * 128:(l + 1) * 128, :], writes=[lb])
    io = ph.sb([128, TT], F32, "iota")
    iob = Buf()
    k.dma(k.qs, io[:], self.iota1, writes=[iob])
    onesT = ph.sb([128, TT], F32, "onesT")
    onesTb = Buf()
    k.op(k.pool, lambda h: h.memset(onesT[:], 1.0), writes=[onesTb])
    sm = ph.sb([128, 16, NPAIR], F32, "sm")
    sb_ = Buf()
    lre, lim, ldt = lam[:, 0:64], lam[:, 64:128], lam[:, 128:192]
    (DT, R, TH, T0, SN, CS, ARE, AIM, AM1, DEN, FRE, FIM, NFRE, NFIM, T1, T2) = [sm[:, i, :] for i in range(16)]
    dve, act = k.dve, k.act

    def V(emit, reads=(), writes=()):
        k.op(dve, emit, reads=list(reads) + [sb_, lb], writes=[sb_])

    def A(emit):
        k.op(act, emit, reads=[sb_, lb], writes=[sb_])

    def VP(emit):
        k.op(k.pool, emit, reads=[sb_, lb], writes=[sb_])

    A(lambda h: h.activation(out=DT, in_=ldt, func=AF.Exp))
    V(lambda h: h.tensor_tensor(out=T0, in0=lre, in1=DT, op=ALU.mult))
    A(lambda h: h.activation(out=R, in_=T0, func=AF.Exp))
    V(lambda h: h.tensor_tensor(out=TH, in0=lim, in1=DT, op=ALU.mult))
    smi = ph.sb([128, NPAIR], I32, "smi")
    V(lambda h: h.tensor_copy(out=T1, in_=TH))
    emit_sin(k, SN, sb_, T1, sb_, T2, sb_, smi[:], sb_)
    V(lambda h: h.tensor_scalar(out=T1, in0=TH, scalar1=0.5 * PI, scalar2=None, op0=ALU.add))
    emit_sin(k, CS, sb_, T1, sb_, T2, sb_, smi[:], sb_)
    V(lambda h: h.tensor_tensor(out=ARE, in0=R, in1=CS, op=ALU.mult))
    V(lambda h: h.tensor_tensor(out=AIM, in0=R, in1=SN, op=ALU.mult))
    V(lambda h: h.tensor_scalar(out=AM1, in0=ARE, scalar1=-1.0, scalar2=None, op0=ALU.add))
    V(lambda h: h.tensor_tensor(out=T1, in0=lre, in1=lre, op=ALU.mult))
    V(lambda h: h.tensor_tensor(out=T2, in0=lim, in1=lim, op=ALU.mult))
    V(lambda h: h.tensor_tensor(out=DEN, in0=T1, in1=T2, op=ALU.add))
    V(lambda h: h.reciprocal(out=DEN, in_=DEN))
    V(lambda h: h.tensor_tensor(out=T1, in0=AM1, in1=lre, op=ALU.mult))
    V(lambda h: h.tensor_tensor(out=T2, in0=AIM, in1=lim, op=ALU.mult))
    V(lambda h: h.tensor_tensor(out=T1, in0=T1, in1=T2, op=ALU.add))
    V(lambda h: h.tensor_tensor(out=FRE, in0=T1, in1=DEN, op=ALU.mult))
    V(lambda h: h.tensor_tensor(out=T1, in0=AIM, in1=lre, op=ALU.mult))
    V(lambda h: h.tensor_tensor(out=T2, in0=AM1, in1=lim, op=ALU.mult))
    V(lambda h: h.tensor_tensor(out=T1, in0=T1, in1=T2, op=ALU.subtract))
    V(lambda h: h.tensor_tensor(out=FIM, in0=T1, in1=DEN, op=ALU.mult))
    V(lambda h: h.tensor_scalar(out=NFRE, in0=FRE, scalar1=-1.0, scalar2=None, op0=ALU.mult))
    V(lambda h: h.tensor_scalar(out=NFIM, in0=FIM, scalar1=-1.0, scalar2=None, op0=ALU.mult))

    cpre = ph.sb([128, NPAIR, 128], BF16, "cpre")
    cpim = ph.sb([128, NPAIR, 128], BF16, "cpim")
    cpb = Buf()
    cld = ph.ring(1, [128, 8, 128], F32, "cld")
    cld2 = ph.ring(1, [128, 8, 128], F32, "cld2")
    ctmp = ph.ring(3, [128, 128], F32, "ctmp")
    cre_src = self.cT[(l * 2 + 0) * 128:(l * 2 + 1) * 128, :].rearrange("p (j c) -> p j c", c=128)
    cim_src = self.cT[(l * 2 + 1) * 128:(l * 2 + 2) * 128, :].rearrange("p (j c) -> p j c", c=128)
    for j0 in range(0, NPAIR, 8):
        cr, crb = cld.next()
        ci, cib = cld2.next()
        k.dma(k.qs, cr[:], cre_src[:, j0:j0 + 8, :], writes=[crb])
        k.dma(k.qs, ci[:], cim_src[:, j0:j0 + 8, :], writes=[cib])
        for jj in range(8):
            j = j0 + jj
            t, tb = ctmp.next()
            k.op(dve, lambda h: h.tensor_scalar(out=t[:], in0=cr[:, jj, :], scalar1=FRE[:, j:j + 1], scalar2=None, op0=ALU.mult),
                 reads=[crb, sb_], writes=[tb])
            k.op(dve, lambda h: h.scalar_tensor_tensor(out=cpre[:, j, :], in0=ci[:, jj, :], scalar=NFIM[:, j:j + 1], in1=t[:],
                                                       op0=ALU.mult, op1=ALU.add), reads=[cib, sb_, tb], writes=[cpb])
            t, tb = ctmp.next()
            k.op(dve, lambda h: h.tensor_scalar(out=t[:], in0=cr[:, jj, :], scalar1=NFIM[:, j:j + 1], scalar2=None, op0=ALU.mult),
                 reads=[crb, sb_], writes=[tb])
            k.op(dve, lambda h: h.scalar_tensor_tensor(out=cpim[:, j, :], in0=ci[:, jj, :], scalar=NFRE[:, j:j + 1], in1=t[:],
                                                       op0=ALU.mult, op1=ALU.add), reads=[cib, sb_, tb], writes=[cpb])

    tabr = ph.ring(1, [128, 12, TT], F32, "tab")
    angr = ph.ring(2, [128, TT], F32, "ang")
    tir = ph.ring(2, [128, TT], I32, "ti")
    bre_r = ph.ring(2, [128, 4, 128], BF16, "bre")
    bim_r = ph.ring(2, [128, 4, 128], BF16, "bim")
    xpr = ph.ring(2, [128, 8], F32, "xp")
    ur = ph.ring(3, [128, TT], F32, "u")
    ubr = ph.ring(3, [128, TT], BF16, "ub")
    dps = ph.psring(4)
    yps = ph.psring(2)
    tmp = ph.ring(8, [128, TT], F32, "tmp")
    wv = ph.ring(3, [128, 4, TT], F32, "wv")
    xf = ph.ring(3, [128, 2, TT], F32, "xf")
    xbr = ph.ring(8, [128, 2, TT], BF16, "xb")
    et = ph.ring(4, [128, TT], F32, "et")
    ob = ph.ring(3, [128, TT], BF16, "ob")
    bview = self.bTb.rearrange("(a p) s -> p a s", p=128)
    pool = k.pool
    for jc in range(NCH):
        tab, tabb = tabr.next()
        for q in range(4):
            j = jc * 4 + q
            for (slot, shift) in ((4 + q, None), (q, 0.5 * PI)):
                ang, angb = angr.next()
                if shift is None:
                    k.op(dve, lambda h: h.tensor_scalar(out=ang[:], in0=io[:], scalar1=TH[:, j:j + 1], scalar2=None, op0=ALU.mult),
                         reads=[iob, sb_], writes=[angb])
                else:
                    k.op(dve, lambda h: h.tensor_scalar(out=ang[:], in0=io[:], scalar1=TH[:, j:j + 1], scalar2=shift,
                                                        op0=ALU.mult, op1=ALU.add), reads=[iob, sb_], writes=[angb])
                t, tb = tmp.next()
                ti, tib = tir.next()
                emit_sin(k, tab[:, slot, :], tabb, ang[:], angb, t[:], tb, ti[:], tib)
            k.op(act, lambda h: h.activation(out=tab[:, 8 + q, :], in_=onesT[:], func=AF.Copy, scale=R[:, j:j + 1]),
                 reads=[onesTb, sb_], writes=[tabb])
        bre, breb = bre_r.next()
        bim, bimb = bim_r.next()
        a0 = (l * 2 + 0) * NPAIR + jc * 4
        a1 = (l * 2 + 1) * NPAIR + jc * 4
        k.dma(k.qs, bre[:], bview[:, a0:a0 + 4, :], writes=[breb])
        k.dma(k.qs, bim[:], bview[:, a1:a1 + 4, :], writes=[bimb])
        xp, xpb = xpr.next()
        k.op(pool, lambda h: h.memset(xp[:], 0.0), writes=[xpb])
        for (t0, tn) in self.tiles:
            u, ub_ = ur.next()
            k.dma(k.qs, u[:, 0:tn], self.uT[jc * 128:(jc + 1) * 128, t0:t0 + tn], writes=[ub_])
            ubf, ubfb = ubr.next()
            k.op(act, lambda h: h.activation(out=ubf[:, 0:tn], in_=u[:, 0:tn], func=AF.Copy), reads=[ub_], writes=[ubfb])
            yp, ypb = yps.next()
            xbs = []
            for q in range(4):
                j = jc * 4 + q
                cs, sn, rt = tab[:, q, 0:tn], tab[:, 4 + q, 0:tn], tab[:, 8 + q, 0:tn]
                dre, dreb = dps.next()
                dim, dimb = dps.next()
                k.mm(dre[:, 0:tn], bre[:, q, :], ubf[:, 0:tn], True, True, reads=[breb, ubfb], writes=[dreb], inc=False)
                k.mm(dim[:, 0:tn], bim[:, q, :], ubf[:, 0:tn], True, True, reads=[bimb, ubfb], writes=[dimb], inc=True)
                w, wb_ = wv.next()
                t1, t1b = tmp.next()
                t2, t2b = tmp.next()
                k.op(dve, lambda h: h.tensor_tensor(out=t1[:, 0:tn], in0=dre[:, 0:tn], in1=cs, op=ALU.mult), reads=[dreb, tabb], writes=[t1b])
                k.op(dve, lambda h: h.tensor_tensor(out=t2[:, 0:tn], in0=dim[:, 0:tn], in1=sn, op=ALU.mult), reads=[dimb, tabb], writes=[t2b])
                k.op(pool, lambda h: h.tensor_tensor(out=w[:, 0, 0:tn], in0=t1[:, 0:tn], in1=t2[:, 0:tn], op=ALU.add), reads=[t1b, t2b], writes=[wb_])
                t3, t3b = tmp.next()
                t4, t4b = tmp.next()
                k.op(dve, lambda h: h.tensor_tensor(out=t3[:, 0:tn], in0=dim[:, 0:tn], in1=cs, op=ALU.mult), reads=[dimb, tabb], writes=[t3b])
                k.op(dve, lambda h: h.tensor_tensor(out=t4[:, 0:tn], in0=dre[:, 0:tn], in1=sn, op=ALU.mult), reads=[dreb, tabb], writes=[t4b])
                k.op(pool, lambda h: h.tensor_tensor(out=w[:, 1, 0:tn], in0=t3[:, 0:tn], in1=t4[:, 0:tn], op=ALU.subtract), reads=[t3b, t4b], writes=[wb_])
                k.op(dve, lambda h: h.tensor_tensor_scan(out=w[:, 2, 0:tn], data0=rt, data1=w[:, 0, 0:tn], initial=xp[:, q:q + 1],
                                                         op0=ALU.mult, op1=ALU.add), reads=[tabb, xpb, wb_], writes=[wb_])
                k.op(dve, lambda h: h.tensor_tensor_scan(out=w[:, 3, 0:tn], data0=rt, data1=w[:, 1, 0:tn], initial=xp[:, 4 + q:5 + q],
                                                         op0=ALU.mult, op1=ALU.add), reads=[tabb, xpb, wb_], writes=[wb_])
                x, xb_ = xf.next()
                t5, t5b = tmp.next()
                t6, t6b = tmp.next()
                k.op(pool, lambda h: h.tensor_tensor(out=t5[:, 0:tn], in0=w[:, 2, 0:tn], in1=cs, op=ALU.mult), reads=[wb_, tabb], writes=[t5b])
                k.op(pool, lambda h: h.tensor_tensor(out=t6[:, 0:tn], in0=w[:, 3, 0:tn], in1=sn, op=ALU.mult), reads=[wb_, tabb], writes=[t6b])
                k.op(pool, lambda h: h.tensor_tensor(out=x[:, 0, 0:tn], in0=t5[:, 0:tn], in1=t6[:, 0:tn], op=ALU.subtract), reads=[t5b, t6b], writes=[xb_])
                t7, t7b = tmp.next()
                t8, t8b = tmp.next()
                k.op(pool, lambda h: h.tensor_tensor(out=t7[:, 0:tn], in0=w[:, 2, 0:tn], in1=sn, op=ALU.mult), reads=[wb_, tabb], writes=[t7b])
                k.op(pool, lambda h: h.tensor_tensor(out=t8[:, 0:tn], in0=w[:, 3, 0:tn], in1=cs, op=ALU.mult), reads=[wb_, tabb], writes=[t8b])
                k.op(pool, lambda h: h.tensor_tensor(out=x[:, 1, 0:tn], in0=t7[:, 0:tn], in1=t8[:, 0:tn], op=ALU.add), reads=[t7b, t8b], writes=[xb_])
                k.op(pool, lambda h: h.tensor_copy(out=xp[:, q:q + 1], in_=x[:, 0, tn - 1:tn]), reads=[xb_], writes=[xpb])
                k.op(pool, lambda h: h.tensor_copy(out=xp[:, 4 + q:5 + q], in_=x[:, 1, tn - 1:tn]), reads=[xb_], writes=[xpb])
                xb16, xb16b = xbr.next()
                k.op(act, lambda h: h.activation(out=xb16[:, :, 0:tn], in_=x[:, :, 0:tn], func=AF.Copy), reads=[xb_], writes=[xb16b])
                xbs.append((xb16, xb16b, j))
            for i, (xb16, xb16b, j) in enumerate(xbs):
                k.mm(yp[:, 0:tn], cpre[:, j, :], xb16[:, 0, 0:tn], i == 0, False, reads=[cpb, xb16b], writes=[ypb], inc=False)
                k.mm(yp[:, 0:tn], cpim[:, j, :], xb16[:, 1, 0:tn], False, i == 3, reads=[cpb, xb16b], writes=[ypb], inc=(i == 3))
            yy, yyb = et.next()
            k.op(dve, lambda h: h.scalar_tensor_tensor(out=yy[:, 0:tn], in0=u[:, 0:tn], scalar=vec[:, V_SD + jc:V_SD + jc + 1],
                                                       in1=yp[:, 0:tn], op0=ALU.mult, op1=ALU.add), reads=[ub_, vb, ypb], writes=[yyb])
            o, o_b = ob.next()
            k.op(act, lambda h: h.activation(out=o[:, 0:tn], in_=yy[:, 0:tn], func=AF.Gelu_apprx_tanh), reads=[yyb], writes=[o_b])
            k.dma(k.qg, self.gsm[jc * 128:(jc + 1) * 128, t0:t0 + tn], o[:, 0:tn], reads=[o_b])
    ph.close()


Prog.phase_ssm = phase_ssm


def _ld_setup(k):
    def setup(ctx):
        ph = ctx["ph"]
        ctx["ld"] = ph.ring(6, [128, TT], F32, "ld")
        ctx["tmp"] = ph.ring(4, [128, TT], F32, "tmp")
        ctx["of"] = ph.ring(3, [128, TT], F32, "of")
        ctx["obf"] = ph.ring(3, [128, TT], BF16, "obf")
    return setup


def _load(self, ctx, src, r0, t0, tn):
    k = self.k
    t, tb = ctx["ld"].next()
    k.dma(k.qs, t[:, 0:tn], src[r0:r0 + 128, t0:t0 + tn], writes=[tb])
    return t, tb


def phase_merge(self, l, which):
    k = self.k
    P = self

    def epi(i):
        def f(ctx, t0, tn, pss):
            g, gb = _load(P, ctx, P.gates, which * D + i * 128, t0, tn)
            if which == 2:
                (pa, pab), (pb_, pbb) = pss
                sg, sgb = ctx["tmp"].next()
                k.op(k.act, lambda h: h.activation(out=sg[:, 0:tn], in_=pb_[:, 0:tn], func=AF.Sigmoid), reads=[pbb], writes=[sgb])
                k.op(k.dve, lambda h: h.tensor_tensor(out=sg[:, 0:tn], in0=pa[:, 0:tn], in1=sg[:, 0:tn], op=ALU.mult),
                     reads=[pab, sgb], writes=[sgb])
                y, yb = sg, sgb
                t, tb = ctx["tmp"].next()
                k.op(k.pool, lambda h: h.tensor_tensor(out=t[:, 0:tn], in0=y[:, 0:tn], in1=g[:, 0:tn], op=ALU.mult),
                     reads=[yb, gb], writes=[tb])
            else:
                (pa, pab), = pss
                t, tb = ctx["tmp"].next()
                k.op(k.dve, lambda h: h.tensor_tensor(out=t[:, 0:tn], in0=pa[:, 0:tn], in1=g[:, 0:tn], op=ALU.mult),
                     reads=[pab, gb], writes=[tb])
            if which > 0:
                m, mb = _load(P, ctx, P.mix, i * 128, t0, tn)
                o, ob = ctx["of"].next()
                k.op(k.pool, lambda h: h.tensor_tensor(out=o[:, 0:tn], in0=t[:, 0:tn], in1=m[:, 0:tn], op=ALU.add),
                     reads=[tb, mb], writes=[ob])
            else:
                o, ob = t, tb
            if which < 2:
                k.dma(k.qg, P.mix[i * 128:(i + 1) * 128, t0:t0 + tn], o[:, 0:tn], reads=[ob])
            else:
                o2, o2b = ctx["obf"].next()
                k.op(k.act, lambda h: h.activation(out=o2[:, 0:tn], in_=o[:, 0:tn], func=AF.Copy), reads=[ob], writes=[o2b])
                k.dma(k.qg, P.mixb[i * 128:(i + 1) * 128, t0:t0 + tn], o2[:, 0:tn], reads=[o2b])
        return f

    if which == 0:
        groups = [([(i * 128, 128)], epi(i)) for i in range(NCH)]
        self.phase_linear(self.oT, NCH, self.wmlab[l * D:(l + 1) * D, :], groups, 3, setup=_ld_setup(k))
    elif which == 1:
        groups = [([(i * 128, 128)], epi(i)) for i in range(NCH)]
        self.phase_linear(self.sT, NCH, self.wconvb[l * D:(l + 1) * D, :], groups, 3, setup=_ld_setup(k))
    else:
        groups = [([(i * 128, 128), (D + i * 128, 128)], epi(i)) for i in range(NCH)]
        self.phase_linear(self.gsm, NCH, self.wglub[l * D:(l + 1) * D, :], groups, 3, setup=_ld_setup(k))


Prog.phase_merge = phase_merge


def phase_resid(self, src, KC, wb, st_n, wcols):
    k = self.k
    P = self

    def epi(i):
        def f(ctx, t0, tn, pss):
            (pa, pab), = pss
            hh, hb_ = _load(P, ctx, P.hT, i * 128, t0, tn)
            o, ob = ctx["of"].next()
            k.op(k.dve, lambda h: h.scalar_tensor_tensor(out=o[:, 0:tn], in0=hh[:, 0:tn], scalar=ALPHA, in1=pa[:, 0:tn],
                                                         op0=ALU.mult, op1=ALU.add), reads=[hb_, pab], writes=[ob])
            k.dma(k.qg, P.pre[i * 128:(i + 1) * 128, t0:t0 + tn], o[:, 0:tn], reads=[ob])
        return f

    groups = [([(i * 128, 128)], epi(i)) for i in range(NCH)]
    self.phase_linear(src, KC, wb, groups, st_n, setup=_ld_setup(k), wcols=wcols)


Prog.phase_resid = phase_resid


def phase_ffn1(self, l):
    k = self.k
    P = self

    def epi(i):
        def f(ctx, t0, tn, pss):
            (pg, pgb), (pu, pub) = pss
            sg, sgb = ctx["tmp"].next()
            k.op(k.act, lambda h: h.activation(out=sg[:, 0:tn], in_=pg[:, 0:tn], func=AF.Silu), reads=[pgb], writes=[sgb])
            o, ob = ctx["obf"].next()
            k.op(k.dve, lambda h: h.tensor_tensor(out=o[:, 0:tn], in0=pu[:, 0:tn], in1=sg[:, 0:tn], op=ALU.mult),
                 reads=[pub, sgb], writes=[ob])
            k.dma(k.qg, P.aT[i * 128:(i + 1) * 128, t0:t0 + tn], o[:, 0:tn], reads=[ob])
        return f

    groups = [([(0, i * 128, 128), (1, i * 128, 128)], epi(i)) for i in range(FCH)]
    self.phase_linear(self.hb, NCH, [self.wfgb[l * D:(l + 1) * D, :], self.wfub[l * D:(l + 1) * D, :]], groups, 3,
                      setup=_ld_setup(k))


Prog.phase_ffn1 = phase_ffn1


def build(self, maxphase=None):
    import os
    if maxphase is None:
        maxphase = int(os.environ.get("MAXPHASE", "100000"))
    cnt = [0]

    def run(f, *a, **kw):
        if cnt[0] < maxphase:
            f(*a, **kw)
        cnt[0] += 1

    run(self.phase_cast)
    run(self.phase_ln, self.xT, self.ln0, 0, 16, dst_f32=self.hT, dst_bf=self.hb)
    for l in range(self.depth):
        last = l == self.depth - 1
        vl = self.vecs[l * 128:(l + 1) * 128, :]
        run(self.phase_inproj, l)
        run(self.phase_kvup, l)
        run(self.phase_attn)
        run(self.phase_conv, l)
        run(self.phase_ssm, l)
        run(self.phase_merge, l, 0)
        run(self.phase_merge, l, 1)
        run(self.phase_merge, l, 2)
        run(self.phase_resid, self.mixb, NCH, self.woutb[l * D:(l + 1) * D, :], 3, 512)
        run(self.phase_ln, self.pre, vl, V_LN1G, V_LN1B, dst_f32=self.hT, dst_bf=self.hb)
        run(self.phase_ffn1, l)
        run(self.phase_resid, self.aT, FCH, self.wfdb[l * FFN:(l + 1) * FFN, :], 1, 256)
        run(self.phase_ln, self.pre, vl, V_LN2G, V_LN2B, dst_f32=None if last else self.hT, dst_bf=None if last else self.hb,
            out_final=self.outT if last else None)
    self.k.barrier()
    return self.nc


Prog.build = build


def _pc(v):
    return np.ascontiguousarray(v.reshape(-1, 128).T)


def host_shared(inp, L, depth):
    f = np.float32
    out = {}
    out["ln0"] = np.concatenate([_pc(inp["ln0_g"]), _pc(inp["ln0_b"])], axis=1).astype(f)
    for nm, key, rows in (("w_in", "w_in", D), ("w_ukv", "w_ukv", RANK), ("w_mla_out", "w_mla_out", D),
                          ("w_conv_out", "w_conv_out", D), ("w_glu", "w_glu", D), ("w_out", "w_out", D),
                          ("w_ffn_gate", "w_ffn_gate", D), ("w_ffn_up", "w_ffn_up", D), ("w_ffn_down", "w_ffn_down", FFN)):
        a = np.asarray(inp[key])[:depth]
        out[nm] = a.reshape(depth * rows, a.shape[-1])
    vecs = np.zeros((depth, 128, NV), f)
    for l in range(depth):
        for off, key in ((V_LN1G, "ln1_g"), (V_LN1B, "ln1_b"), (V_LN2G, "ln2_g"), (V_LN2B, "ln2_b"), (V_CB, "conv_b"),
                         (V_CLG, "conv_ln_g"), (V_CLB, "conv_ln_b"), (V_SD, "ssm_d"), (V_BG, "b_gate"), (V_KVG, "kv_norm_g")):
            v = _pc(np.asarray(inp[key])[l])
            vecs[l, :, off:off + v.shape[1]] = v
    out["vecs"] = vecs.reshape(depth * 128, NV)
    cw = np.asarray(inp["conv_w"])[:depth]
    out["convw"] = np.ascontiguousarray(cw.reshape(depth, CONV_K, NCH, 128).transpose(0, 3, 2, 1)).reshape(depth * 128, NCH * CONV_K)
    lam = np.zeros((depth, 128, 3 * NPAIR), f)
    bT = np.zeros((depth, 2, NPAIR, 128, 128), f)
    cT = np.zeros((depth, 2, 128, NPAIR, 128), f)
    for l in range(depth):
        lre = np.asarray(inp["ssm_lam_re"])[l].reshape(NPAIR, 2, SSM_P)
        lim = np.asarray(inp["ssm_lam_im"])[l].reshape(NPAIR, 2, SSM_P)
        ldt = np.asarray(inp["ssm_log_dt"])[l].reshape(NPAIR, 2)
        lam[l, :, 0:64] = lre.transpose(1, 2, 0).reshape(128, NPAIR)
        lam[l, :, 64:128] = lim.transpose(1, 2, 0).reshape(128, NPAIR)
        lam[l, :, 128:192] = np.broadcast_to(ldt.T[:, None, :], (2, SSM_P, NPAIR)).reshape(128, NPAIR)
        for ri, (bk, ck) in enumerate((("ssm_b_re", "ssm_c_re"), ("ssm_b_im", "ssm_c_im"))):
            b_ = np.asarray(inp[bk])[l].reshape(NPAIR, 2, SSM_P, 16)
            c_ = np.asarray(inp[ck])[l].reshape(NPAIR, 2, 16, SSM_P)
            for q in range(4):
                for gg in range(2):
                    r0 = q * 32 + gg * 16
                    bT[l, ri, q::4, r0:r0 + 16, gg * 64:(gg + 1) * 64] = b_[q::4, gg].transpose(0, 2, 1)
                    cT[l, ri, gg * 64:(gg + 1) * 64, q::4, r0:r0 + 16] = c_[q::4, gg].transpose(2, 0, 1)
    out["lam"] = lam.reshape(depth * 128, 3 * NPAIR)
    out["bT"] = bT.reshape(depth * 2 * NPAIR * 128, 128)
    out["cT"] = cT.reshape(depth * 2 * 128, NPAIR * 128)
    half = ROPE // 2
    inv = np.power(np.float32(10000.0), -np.arange(half, dtype=f) / np.float32(half)).astype(f)
    ang = (np.arange(L, dtype=f)[:, None] * inv[None, :]).astype(f)
    out["cosr"] = np.ascontiguousarray(np.tile(np.cos(ang).astype(f).T, (4, 1)))
    out["sinr"] = np.ascontiguousarray(np.tile(np.sin(ang).astype(f).T, (4, 1)))
    kk = np.arange(128)[:, None]
    qq = np.arange(TT)[None, :]
    masks = np.stack([((2 * jj + kk // 64) <= (qq // 64)) for jj in range(4)]).astype(f)
    out["masks"] = masks.reshape(4 * 128, TT).astype(ml_dtypes.bfloat16)
    out["iota1"] = np.broadcast_to(np.arange(1, TT + 1, dtype=f)[None, :], (128, TT)).copy()
    return out


def host_xT(inp, b, L):
    x = np.asarray(inp["x"])[b, :L - N_META]
    cat = np.concatenate([np.asarray(inp["meta_tokens"]), x], axis=0)
    return np.ascontiguousarray(cat.T.astype(np.float32))


def kernel(**inputs):
    L = inputs["x"].shape[1] + N_META
    depth = inputs["w_in"].shape[0]
    B = inputs["x"].shape[0]
    prog = Prog(L, depth)
    nc = prog.build()
    shared = host_shared(inputs, L, depth)
    ncores = 8
    in_maps = []
    for c in range(ncores):
        m = dict(shared)
        m["xT"] = host_xT(inputs, (c * B) // ncores, L)
        in_maps.append(m)
    res = run_bass_kernel_spmd(nc, in_maps, core_ids=list(range(ncores)))
    out = np.empty((B, L - N_META, D), np.float32)
    per = ncores // B
    for b in range(B):
        out[b] = res.results[b * per]["outT"].T
    return out
```
